# Optimizing a Trainium2 kernel written in Bass

```python
import jax, jax.numpy as jnp
from jax import lax
import numpy as np

D_MODEL = 2048
BATCH = 2
SEQ = 4096
DEPTH = 4

N_MIXERS = 3
EPS = 1e-6
SB_HEADS = 16
SB_HEAD_DIM = D_MODEL // SB_HEADS
SB_WIDTH = SB_HEADS * SB_HEAD_DIM
Q_BLOCK = 128
GM_CHUNK = 128
GM_GROUPS = 16
GM_WIDTH = D_MODEL
GM_GROUP_DIM = GM_WIDTH // GM_GROUPS
SSD_INNER = 2 * D_MODEL
SSD_HEAD_DIM = 64
SSD_HEADS = SSD_INNER // SSD_HEAD_DIM
SSD_GROUPS = 8
SSD_HPG = SSD_HEADS // SSD_GROUPS
SSD_STATE = 128
SSD_CONV = 4
SSD_CHUNK = 128
SSD_CONV_DIM = SSD_INNER + 2 * SSD_GROUPS * SSD_STATE
SSD_PROJ = SSD_INNER + SSD_CONV_DIM + SSD_HEADS
MLP_HIDDEN = 4 * D_MODEL
N_A = len(range(0, DEPTH, N_MIXERS))
N_B = len(range(1, DEPTH, N_MIXERS))
N_C = len(range(2, DEPTH, N_MIXERS))

kernel_name = "hybrid_sb_gmlp_ssd_trunk"


def rms_norm(x, g):
    xf = x.astype(jnp.float32)
    y = xf * lax.rsqrt(jnp.mean(xf * xf, axis=-1, keepdims=True) + EPS)
    return (y * g.astype(jnp.float32)).astype(x.dtype)


def stick_breaking_attention(h, w_qkv, q_g, k_g, w_o):
    b, s, _ = h.shape
    q, k, v = jnp.split(h @ w_qkv, 3, axis=-1)
    q = rms_norm(q.reshape(b, s, SB_HEADS, SB_HEAD_DIM), q_g).transpose(0, 2, 1, 3)
    k = rms_norm(k.reshape(b, s, SB_HEADS, SB_HEAD_DIM), k_g).transpose(0, 2, 1, 3)
    v = v.reshape(b, s, SB_HEADS, SB_HEAD_DIM).transpose(0, 2, 1, 3)
    scale = SB_HEAD_DIM ** -0.5
    n_blk = s // Q_BLOCK
    q_blocks = q.reshape(b, SB_HEADS, n_blk, Q_BLOCK, SB_HEAD_DIM).transpose(2, 0, 1, 3, 4)
    k_pos = jnp.arange(s)

    def one_block(args):
        qb, start = args
        z = jnp.einsum('bhqd,bhkd->bhqk', qb, k).astype(jnp.float32) * scale
        q_pos = start + jnp.arange(Q_BLOCK)
        mask = k_pos[None, :] < q_pos[:, None]
        log_beta = jax.nn.log_sigmoid(z)
        log_1m = jnp.where(mask, jax.nn.log_sigmoid(-z), 0.0)
        suffix = lax.cumsum(log_1m, axis=3, reverse=True) - log_1m
        a = jnp.where(mask, jnp.exp(log_beta + suffix), 0.0)
        return jnp.einsum('bhqk,bhkd->bhqd', a.astype(v.dtype), v)

    starts = jnp.arange(n_blk, dtype=jnp.int32) * Q_BLOCK
    o = lax.map(one_block, (q_blocks, starts))
    o = o.transpose(1, 0, 3, 2, 4).reshape(b, s, SB_WIDTH)
    return o @ w_o


def chunked_spatial_gating(h, w_in, v_g, w_s, b_s, w_o):
    b, s, _ = h.shape
    u, v = jnp.split(jax.nn.gelu(h @ w_in, approximate=False), 2, axis=-1)
    v = rms_norm(v, v_g)
    n_chunk = s // GM_CHUNK
    v = v.reshape(b, n_chunk, GM_CHUNK, GM_GROUPS, GM_GROUP_DIM)
    u = u.reshape(b, n_chunk, GM_CHUNK, GM_GROUPS, GM_GROUP_DIM)
    causal = jnp.tril(jnp.ones((GM_CHUNK, GM_CHUNK), dtype=bool))
    w = jnp.where(causal[None], w_s, 0.0)
    mixed = jnp.einsum('gts,bcsgd->bctgd', w.astype(v.dtype), v) + b_s.T[:, :, None]
    y = (u * mixed).reshape(b, s, GM_WIDTH)
    return y @ w_o


def ssd_chunked_scan(x, dt, a, bm, cm):
    b, s = x.shape[:2]
    c, L = s // SSD_CHUNK, SSD_CHUNK
    xs = (x * dt[..., None]).reshape(b, c, L, SSD_GROUPS, SSD_HPG, SSD_HEAD_DIM)
    da = (dt * a).reshape(b, c, L, SSD_GROUPS, SSD_HPG).transpose(0, 3, 4, 1, 2)
    bc = bm.reshape(b, c, L, SSD_GROUPS, SSD_STATE)
    cc = cm.reshape(b, c, L, SSD_GROUPS, SSD_STATE)
    a_cum = jnp.cumsum(da, axis=-1)
    seg = a_cum[..., :, None] - a_cum[..., None, :]
    tri = jnp.tril(jnp.ones((L, L), dtype=bool))
    decay = jnp.exp(jnp.where(tri, seg, -jnp.inf))
    cb = jnp.einsum('bclgn,bcsgn->bgcls', cc, bc)
    y_diag = jnp.einsum('bgcls,bgrcls,bcsgrp->bclgrp', cb, decay, xs)
    decay_states = jnp.exp(a_cum[..., -1:] - a_cum)
    states = jnp.einsum('bclgn,bgrcl,bclgrp->cbgrpn', bc, decay_states, xs)
    chunk_decay = jnp.exp(a_cum[..., -1]).transpose(3, 0, 1, 2)

    def step(carry, inp):
        st, dec = inp
        return carry * dec[..., None, None] + st, carry

    _, prev = lax.scan(step, jnp.zeros_like(states[0]), (states, chunk_decay))
    y_off = jnp.einsum('bclgn,cbgrpn,bgrcl->bclgrp', cc, prev, jnp.exp(a_cum))
    return (y_diag + y_off).reshape(b, s, SSD_GROUPS, SSD_HPG, SSD_HEAD_DIM)


def ssd_mixer(h, w_in, conv_w, conv_b, dt_bias, a_log, d_skip, norm_g, w_o):
    b, s, _ = h.shape
    z, xbc, dt = jnp.split(h @ w_in, [SSD_INNER, SSD_INNER + SSD_CONV_DIM], axis=-1)
    xbc = lax.conv_general_dilated(xbc, conv_w[:, None, :].astype(xbc.dtype), window_strides=(1,),
                                   padding=[(SSD_CONV - 1, 0)],
                                   dimension_numbers=('NWC', 'WIO', 'NWC'),
                                   feature_group_count=SSD_CONV_DIM) + conv_b
    xbc = jax.nn.silu(xbc)
    xi, bm, cm = jnp.split(xbc, [SSD_INNER, SSD_INNER + SSD_GROUPS * SSD_STATE], axis=-1)
    f32 = jnp.float32
    dt = jax.nn.softplus(dt.astype(f32) + dt_bias.astype(f32))
    a = -jnp.exp(a_log.astype(f32))
    xi = xi.astype(f32).reshape(b, s, SSD_GROUPS, SSD_HPG, SSD_HEAD_DIM)
    y = ssd_chunked_scan(xi, dt.reshape(b, s, SSD_GROUPS, SSD_HPG), a.reshape(SSD_GROUPS, SSD_HPG),
                         bm.astype(f32).reshape(b, s, SSD_GROUPS, SSD_STATE),
                         cm.astype(f32).reshape(b, s, SSD_GROUPS, SSD_STATE))
    y = y + d_skip.astype(f32).reshape(SSD_GROUPS, SSD_HPG)[:, :, None] * xi
    y = y.reshape(b, s, SSD_INNER).astype(h.dtype) * jax.nn.silu(z)
    y = rms_norm(y.reshape(b, s, SSD_GROUPS, SSD_INNER // SSD_GROUPS),
                 norm_g.reshape(SSD_GROUPS, SSD_INNER // SSD_GROUPS)).reshape(b, s, SSD_INNER)
    return y @ w_o


def squared_relu_mlp(h, w_in, w_out):
    return jnp.square(jax.nn.relu(h @ w_in)) @ w_out


def setup_inputs(seed: int = 0) -> dict:
    key = jax.random.key(seed)
    ks = jax.random.split(key, 24)
    nrm = jax.random.normal
    f32 = jnp.float32
    d = D_MODEL
    dt = jnp.exp(jax.random.uniform(ks[14], (N_C, SSD_HEADS), f32, np.log(1e-3), np.log(1e-1)))
    return {
        "x": nrm(ks[0], (BATCH, SEQ, d), f32),
        "norm_mix_g": 1.0 + 0.02 * nrm(ks[1], (DEPTH, d), f32),
        "norm_mlp_g": 1.0 + 0.02 * nrm(ks[2], (DEPTH, d), f32),
        "sb_w_qkv": nrm(ks[3], (N_A, d, 3 * SB_WIDTH), f32) * d ** -0.5,
        "sb_q_norm_g": 1.0 + 0.02 * nrm(ks[4], (N_A, SB_HEAD_DIM), f32),
        "sb_k_norm_g": 1.0 + 0.02 * nrm(ks[5], (N_A, SB_HEAD_DIM), f32),
        "sb_w_o": nrm(ks[6], (N_A, SB_WIDTH, d), f32) * SB_WIDTH ** -0.5,
        "gm_w_in": nrm(ks[7], (N_B, d, 2 * GM_WIDTH), f32) * d ** -0.5,
        "gm_v_norm_g": 1.0 + 0.02 * nrm(ks[8], (N_B, GM_WIDTH), f32),
        "gm_w_s": nrm(ks[9], (N_B, GM_GROUPS, GM_CHUNK, GM_CHUNK), f32) * (1.0 / GM_CHUNK),
        "gm_b_s": 1.0 + 0.02 * nrm(ks[10], (N_B, GM_GROUPS, GM_CHUNK), f32),
        "gm_w_o": nrm(ks[11], (N_B, GM_WIDTH, d), f32) * GM_WIDTH ** -0.5,
        "ssd_w_in": nrm(ks[12], (N_C, d, SSD_PROJ), f32) * d ** -0.5,
        "ssd_conv_w": nrm(ks[13], (N_C, SSD_CONV, SSD_CONV_DIM), f32) * SSD_CONV ** -0.5,
        "ssd_conv_b": 0.02 * nrm(ks[15], (N_C, SSD_CONV_DIM), f32),
        "ssd_dt_bias": dt + jnp.log(-jnp.expm1(-dt)),
        "ssd_a_log": jnp.log(jax.random.uniform(ks[16], (N_C, SSD_HEADS), f32, 1.0, 16.0)),
        "ssd_d": 1.0 + 0.02 * nrm(ks[17], (N_C, SSD_HEADS), f32),
        "ssd_norm_g": 1.0 + 0.02 * nrm(ks[18], (N_C, SSD_INNER), f32),
        "ssd_w_o": nrm(ks[19], (N_C, SSD_INNER, d), f32) * SSD_INNER ** -0.5,
        "mlp_w_in": nrm(ks[20], (DEPTH, d, MLP_HIDDEN), f32) * d ** -0.5,
        "mlp_w_out": nrm(ks[21], (DEPTH, MLP_HIDDEN, d), f32) * MLP_HIDDEN ** -0.5,
    }


def reference(x, norm_mix_g, norm_mlp_g, sb_w_qkv, sb_q_norm_g, sb_k_norm_g, sb_w_o,
              gm_w_in, gm_v_norm_g, gm_w_s, gm_b_s, gm_w_o,
              ssd_w_in, ssd_conv_w, ssd_conv_b, ssd_dt_bias, ssd_a_log, ssd_d, ssd_norm_g, ssd_w_o,
              mlp_w_in, mlp_w_out):
    h = x
    for i in range(DEPTH):
        kind, j = i % N_MIXERS, i // N_MIXERS
        hn = rms_norm(h, norm_mix_g[i])
        if kind == 0:
            mix = stick_breaking_attention(hn, sb_w_qkv[j], sb_q_norm_g[j], sb_k_norm_g[j], sb_w_o[j])
        elif kind == 1:
            mix = chunked_spatial_gating(hn, gm_w_in[j], gm_v_norm_g[j], gm_w_s[j], gm_b_s[j], gm_w_o[j])
        else:
            mix = ssd_mixer(hn, ssd_w_in[j], ssd_conv_w[j], ssd_conv_b[j], ssd_dt_bias[j],
                            ssd_a_log[j], ssd_d[j], ssd_norm_g[j], ssd_w_o[j])
        h = h + mix
        h = h + squared_relu_mlp(rms_norm(h, norm_mlp_g[i]), mlp_w_in[i], mlp_w_out[i])
    return h
```

```python
import contextlib
import numpy as np
import concourse.bass as bass
import concourse.mybir as mybir
from concourse.bass_utils import run_bass_kernel_spmd

F32 = mybir.dt.float32
BF16 = mybir.dt.bfloat16
I32 = mybir.dt.int32
AF = mybir.ActivationFunctionType
ALU = mybir.AluOpType

NCORES = 8
D = 2048
TOK = 1024
NT = 8
SEQ = 4096
EPS = 1e-6
DEPTH = 4
ENGS = ("pe", "act", "dve", "pool", "sp")


class Op:
    __slots__ = ("idx", "eng", "fn", "deps", "dma_key", "signal", "ev", "inc", "epoch")

    def __init__(self, idx, eng, fn, dma_key, inc, epoch=0):
        self.epoch = epoch
        self.idx = idx
        self.eng = eng
        self.fn = fn
        self.deps = set()
        self.dma_key = dma_key
        self.signal = dma_key is not None
        self.ev = None
        self.inc = inc


class Prog:
    def __init__(self, nc):
        self.nc = nc
        self.ops = []
        self.last_w = {}
        self.readers = {}
        self.epoch = 0

    def op(self, eng, fn, reads=(), writes=(), dma=None, inc=16):
        o = Op(len(self.ops), eng, fn, dma, inc, self.epoch)
        self.ops.append(o)
        deps = o.deps
        for k in reads:
            w = self.last_w.get(k)
            if w is not None:
                deps.add(w)
        for k in writes:
            w = self.last_w.get(k)
            if w is not None:
                deps.add(w)
            rd = self.readers.get(k)
            if rd:
                deps.update(rd.values())
        for k in reads:
            self.readers.setdefault(k, {})[(eng, dma)] = o.idx
        for k in writes:
            self.last_w[k] = o.idx
            self.readers[k] = {}
        deps.discard(o.idx)
        return o

    def pe(self, fn, reads=(), writes=()):
        return self.op("pe", fn, reads, writes)

    def act(self, fn, reads=(), writes=()):
        return self.op("act", fn, reads, writes)

    def dve(self, fn, reads=(), writes=()):
        return self.op("dve", fn, reads, writes)

    def dma(self, queue, key, fn, reads=(), writes=(), inc=16):
        return self.op(queue, fn, reads, writes, dma=key, inc=inc)

    def emit(self, final_wait_ops=()):
        nc = self.nc
        ops = self.ops
        for o in ops:
            if o.eng == "pe" and o.dma_key is None:
                for d in list(o.deps):
                    p = ops[d]
                    if p.dma_key is None and p.eng == "pe":
                        o.deps.discard(d)
        for o in ops:
            for d in o.deps:
                ops[d].signal = True
        for i in final_wait_ops:
            ops[i].signal = True
        eng_cnt = {}
        dma_cnt = {}
        dma_keys = []
        for o in ops:
            if o.dma_key is not None:
                if o.dma_key not in dma_cnt:
                    dma_cnt[o.dma_key] = 0
                    dma_keys.append(o.dma_key)
                dma_cnt[o.dma_key] += o.inc
                o.ev = (("dma", o.dma_key), dma_cnt[o.dma_key])
            elif o.signal:
                ek = (o.eng, o.epoch)
                eng_cnt[ek] = eng_cnt.get(ek, 0) + 1
                o.ev = (("eng", o.eng, o.epoch), eng_cnt[ek])
        sem_names = [("eng", e, ep) for (e, ep) in eng_cnt] + [("dma", k) for k in dma_keys]
        self.n_sems = len(sem_names)
        self.eng_cnt = eng_cnt
        with contextlib.ExitStack() as st:
            sems = {}
            for i, sn in enumerate(sem_names):
                sems[sn] = st.enter_context(nc.semaphore("s%d" % i))
            block = st.enter_context(nc.Block())
            per_eng = {e: [o for o in ops if o.eng == e] for e in ENGS}

            def run_stream(e, engobj, extra_final=False):
                seen = {}
                for o in per_eng[e]:
                    waits = {}
                    for d in o.deps:
                        sn, v = ops[d].ev
                        if seen.get(sn, 0) >= v:
                            continue
                        if waits.get(sn, 0) < v:
                            waits[sn] = v
                    for sn, v in waits.items():
                        engobj.wait_ge(sems[sn], v)
                        seen[sn] = v
                    ins = o.fn(engobj)
                    if o.dma_key is not None:
                        ins.then_inc(sems[("dma", o.dma_key)], o.inc)
                    elif o.signal:
                        ins.then_inc(sems[("eng", e, o.epoch)], 1)
                if extra_final:
                    for i in final_wait_ops:
                        sn, v = ops[i].ev
                        if seen.get(sn, 0) < v:
                            engobj.wait_ge(sems[sn], v)
                            seen[sn] = v

            @block.tensor
            def _(eng):
                run_stream("pe", eng)

            @block.scalar
            def _(eng):
                run_stream("act", eng)

            @block.vector
            def _(eng):
                run_stream("dve", eng)

            @block.gpsimd
            def _(eng):
                run_stream("pool", eng)

            @block.sync
            def _(eng):
                run_stream("sp", eng, extra_final=True)


def tile_weight(W, KG, CW):
    K, N = W.shape
    nkb = K // (128 * KG)
    ncb = N // CW
    t = W.reshape(nkb, KG, 128, ncb, CW).transpose(0, 3, 2, 1, 4)
    return np.ascontiguousarray(t).reshape(nkb * ncb, 128, KG * CW)


class GW:
    def __init__(self):
        self.n = 0
        self.tab = {}
        self.chunks = []

    def add(self, name, tiles):
        self.tab[name] = (self.n, tiles.shape[0])
        self.n += tiles.shape[0]
        self.chunks.append(tiles)


def gathered_layout(inp, layers):
    gw = GW()
    for li in layers:
        kind, j = li % 3, li // 3
        if kind == 0:
            gw.add(("wo", li), tile_weight(inp["sb_w_o"][j], 4, 2048))
        elif kind == 1:
            gw.add(("gin", li), tile_weight(inp["gm_w_in"][j], 16, 512))
            gw.add(("wo", li), tile_weight(inp["gm_w_o"][j], 4, 2048))
        else:
            gw.add(("wo", li), tile_weight(inp["ssd_w_o"][j], 4, 2048))
        win = tile_weight(inp["mlp_w_in"][li], 16, 512)
        wout = tile_weight(inp["mlp_w_out"][li], 4, 2048)
        inter = []
        for hb in range(8):
            inter += [win[2 * hb], win[2 * hb + 1], wout[2 * hb], wout[2 * hb + 1]]
        gw.add(("mlp", li), np.stack(inter))
    return gw


class K:
    pass


class StopBuild(Exception):
    pass


import os
STOP = int(os.environ.get("KSTOP", "0"))


def ckpt(level):
    if STOP and level >= STOP:
        raise StopBuild()


def build_program(layers, gw_tab, n_gtiles, n_cols, dbg=False):
    if isinstance(layers, int):
        layers = list(range(layers))
    att_layers = [li for li in layers if li % 3 == 0]
    nc = bass.Bass("TRN2", target_bir_lowering=False)
    k = K()
    k.nc = nc
    P = Prog(nc)
    k.P = P

    def ext(name, shape, dt=F32):
        return nc.dram_tensor(name, list(shape), dt, kind="ExternalInput").ap()

    x_in = ext("x", [TOK, D])
    wg_in = ext("wg", [n_gtiles * 128, 1024])
    gvec = ext("gvec", [2 * DEPTH + 1, D])
    cols_in = ext("cols", [128, n_cols])
    idx_in = ext("idx", [128, 48], I32)
    n_att = len(att_layers)
    wqkv_in = ext("wqkv", [max(n_att, 1) * 4 * 128, 6144])
    gws_in = ext("gws", [128, 16 * 128])
    wssd_in = ext("wssd", [6 * 128, 8192])
    wssd_bf = nc.dram_tensor("wssdb", [6 * 128, 8192], BF16).ap()
    zx = nc.dram_tensor("zx", [21 * 128, SEQ], F32).ap()
    ygd = nc.dram_tensor("ygd", [1024, SEQ], BF16).ap()
    y_out = nc.dram_tensor("y", [TOK, D], F32, kind="ExternalOutput").ap()

    ngroups = n_gtiles // 2
    wb32 = nc.dram_tensor("wb32", [n_gtiles * 128, 512], F32)
    GPT = 32
    wgg32_l = [nc.dram_tensor("wgg32_%d" % q, [min(GPT, ngroups - q * GPT) * 8 * 256, 512], F32)
               for q in range((ngroups + GPT - 1) // GPT)]
    wb_bf = wb32.ap().bitcast(BF16)
    wgg_bf_l = [t_.ap().bitcast(BF16) for t_ in wgg32_l]
    wqkv_bf = nc.dram_tensor("wqkvb", [max(n_att, 1) * 4 * 128, 6144], BF16).ap()
    hx_b32 = nc.dram_tensor("hxb", [2048, 512], F32)
    hx_g32 = nc.dram_tensor("hxg", [8 * 4 * 256, 512], F32)
    ob_b32 = nc.dram_tensor("obb", [1024, 2048], F32)
    ob_g32 = nc.dram_tensor("obg", [16 * 4 * 64, 2048], F32)
    hx_b = hx_b32.ap().bitcast(BF16)
    hx_g = hx_g32.ap().bitcast(BF16)
    ob_b = ob_b32.ap().bitcast(BF16)
    ob_g = ob_g32.ap().bitcast(BF16)

    with contextlib.ExitStack() as st:
        def sb(name, shape, dt):
            return st.enter_context(nc.sbuf_tensor(name, list(shape), dt))

        h = sb("h", [128, NT, D], F32)
        hnT = sb("hnT", [128, 16, TOK], BF16)
        work = sb("work", [128, 32768], BF16)
        NWS = 2
        wsl = [sb("w%d" % i, [128, 8192], BF16) for i in range(NWS)]

        def wk(off, n):
            return [("wk", g) for g in range(off // 512, (off + n - 1) // 512 + 1)]

        GBC_OFF, HNTOK_OFF, JUNK_OFF = 22528, 26624, 30720
        gbc = work[:, GBC_OFF:GBC_OFF + 4096].bitcast(F32)
        hntok = [work[:, HNTOK_OFF + i * 2048:HNTOK_OFF + (i + 1) * 2048] for i in range(2)]
        junk = work[:, JUNK_OFF:JUNK_OFF + 2048]
        k_gbc = wk(GBC_OFF, 4096)
        k_hntok = [wk(HNTOK_OFF + i * 2048, 2048) for i in range(2)]
        k_junk = wk(JUNK_OFF, 2048)
        ident = sb("ident", [128, 128], BF16)
        jrev = sb("jrev", [128, 128], BF16)
        ones_bf = sb("onesbf", [128, 128], BF16)
        ones_f = sb("onesf", [128, 512], F32)
        mdiag = sb("mdiag", [128, 128], F32)
        tmpf = sb("tmpf", [128, 128], F32)
        epsc = sb("epsc", [128, 1], F32)
        ss = sb("ss", [128, 16], F32)
        rs = sb("rs", [128, 16], F32)
        cols = sb("cols_sb", [128, n_cols], F32)
        idxt = sb("idxt_sb", [128, 48], I32)
        relu_t = [sb("relu%d" % i, [128, 512], F32) for i in range(2)]
        ps = [st.enter_context(nc.psum_tensor("ps%d" % i, [128, 512], F32)) for i in range(8)]
        k.h, k.hnT, k.work, k.wsl, k.ps = h, hnT, work, wsl, ps

        bank_ctr = [0]

        def bank(pool=None):
            if pool is None:
                pool = range(8)
            pool = list(pool)
            b = pool[bank_ctr[0] % len(pool)]
            bank_ctr[0] += 1
            return b

        evac_ctr = [0]

        def evac_copy(out_ap, in_ap, reads, writes, eng=None):
            if eng is None:
                eng = "act" if evac_ctr[0] % 2 == 0 else "dve"
                evac_ctr[0] += 1
            if eng == "act":
                P.act(lambda e: e.activation(out=out_ap, in_=in_ap, func=AF.Copy), reads, writes)
            else:
                P.dve(lambda e: e.tensor_copy(out=out_ap, in_=in_ap), reads, writes)

        def mk_const():
            P.op("pool", lambda e: e.memset(tmpf[:], 1.0), writes=["tmpf"])
            P.op("pool", lambda e: e.affine_select(out=mdiag[:], in_=tmpf[:], pattern=[[-1, 128]],
                                                   compare_op=ALU.is_equal, fill=0.0, base=0, channel_multiplier=1),
                 reads=["tmpf"], writes=["mdiag"])
            P.dve(lambda e: e.tensor_copy(out=ident[:], in_=mdiag[:]), reads=["mdiag"], writes=["ident"])
            P.op("pool", lambda e: e.affine_select(out=mdiag[:], in_=tmpf[:], pattern=[[1, 128]],
                                                   compare_op=ALU.is_equal, fill=0.0, base=-127, channel_multiplier=1),
                 reads=["tmpf", "ident"], writes=["mdiag"])
            P.dve(lambda e: e.tensor_copy(out=jrev[:], in_=mdiag[:]), reads=["mdiag"], writes=["jrev"])
            P.op("pool", lambda e: e.affine_select(out=mdiag[:], in_=tmpf[:], pattern=[[1, 128]],
                                                   compare_op=ALU.is_gt, fill=0.0, base=-127, channel_multiplier=1),
                 reads=["tmpf", "jrev"], writes=["mdiag"])
            P.dve(lambda e: e.memset(ones_bf[:], 1.0), writes=["onesbf"])
            P.dve(lambda e: e.memset(epsc[:], EPS), writes=["epsc"])
            P.dve(lambda e: e.memset(ones_f[:], 1.0), writes=["onesf"])
            P.dma("sp", "cols", lambda e: e.dma_start(out=cols[:], in_=cols_in[:, :]), writes=["cols"])
            P.dma("sp", "idxt", lambda e: e.dma_start(out=idxt[:], in_=idx_in[:, :]), writes=["idxt"])

        prepped = set()

        def prep_group(g):
            if g in prepped or g >= ngroups:
                return
            prepped.add(g)
            r0 = g * 256
            P.dma("pool", "cast", lambda e: e.dma_start(out=wb_bf[r0:r0 + 256, :], in_=wg_in[r0:r0 + 256, :]),
                  writes=[("wb", g)])
            P.dma("pool", "cc", lambda e: e.collective_compute(
                "AllGather", ALU.bypass, replica_groups=[list(range(8))],
                ins=[wb32[r0:r0 + 256, :]], outs=[wgg32_l[g // GPT][(g % GPT) * 2048:(g % GPT + 1) * 2048, :]]),
                reads=[("wb", g)], writes=[("wgg", g)], inc=1)

        def prep_tiles(t0, n):
            for g in range(t0 // 2, (t0 + n + 1) // 2):
                prep_group(g)

        wslot_ctr = [0]

        def load_gtile(t):
            s = wslot_ctr[0] % NWS
            wslot_ctr[0] += 1
            g, tl = t // 2, t % 2
            src = wgg_bf_l[g // GPT][(g % GPT) * 2048:(g % GPT + 1) * 2048, :].rearrange(
                "(r t p) f -> t p r f", r=8, t=2, p=128)[tl]
            dst = wsl[s][:, :].rearrange("p (r f) -> p r f", r=8)
            P.dma("sp", ("w", s), lambda e: e.dma_start(out=dst, in_=src), reads=[("wgg", g)], writes=[("w", s)])
            return wsl[s], ("w", s)

        def norm_stage(grow):
            P.dma("sp", "gbc", lambda e: e.dma_start(out=gbc, in_=gvec[grow:grow + 1, :].partition_broadcast(128)),
                  writes=k_gbc)
            for i in range(NT):
                P.act(lambda e, i=i: e.activation(out=junk, in_=h[:, i, :], func=AF.Square,
                                                  accum_out=ss[:, i:i + 1]),
                      reads=[("h", i)], writes=[("ss", i)] + k_junk)
                P.act(lambda e, i=i: e.activation(out=rs[:, i:i + 1], in_=ss[:, i:i + 1], func=AF.Sqrt, scale=1.0 / D,
                                                  bias=epsc[:, 0:1]), reads=[("ss", i), "epsc"], writes=[("rs", i)])
                P.dve(lambda e, i=i: e.reciprocal(out=rs[:, i:i + 1], in_=rs[:, i:i + 1]),
                      reads=[("rs", i)], writes=[("rs", i)])
                ht = hntok[i % 2]
                kht = k_hntok[i % 2]
                P.dve(lambda e, i=i, ht=ht: e.scalar_tensor_tensor(out=ht, in0=h[:, i, :], scalar=rs[:, i:i + 1], in1=gbc,
                                                                   op0=ALU.mult, op1=ALU.mult),
                      reads=[("h", i), ("rs", i)] + k_gbc, writes=kht)
                for cq in range(4):
                    b = bank()
                    for c in range(4):
                        cc = cq * 4 + c
                        P.pe(lambda e, b=b, c=c, cc=cc, ht=ht: e.matmul(ps[b][:, c * 128:(c + 1) * 128],
                                                                         lhsT=ht[:, cc * 128:(cc + 1) * 128], rhs=ident[:],
                                                                         start=True, stop=True),
                             reads=kht + ["ident"], writes=[("ps", b)])
                    evac_copy(hnT[:, cq * 4:(cq + 1) * 4, i * 128:(i + 1) * 128],
                              ps[b][:, :].rearrange("p (c t) -> p c t", c=4),
                              [("ps", b)], [("hnT", cq * 4 + c2, i) for c2 in range(4)])

        def hnT_keys(kc=None, tiles=range(NT)):
            kcs = range(16) if kc is None else [kc]
            return [("hnT", a, i) for a in kcs for i in tiles]

        def proj_out(src_fn, src_keys_fn, tile_ids):
            for ti, t in enumerate(tile_ids):
                wt, wkey = load_gtile(t)
                for i in range(NT):
                    for n in range(4):
                        b = bank()
                        for fc in range(4):
                            f = ti * 4 + fc
                            P.pe(lambda e, b=b, f=f, fc=fc, i=i, n=n, wt=wt: e.matmul(
                                ps[b][:, :], lhsT=src_fn(f, i), rhs=wt[:, fc * 2048 + n * 512: fc * 2048 + (n + 1) * 512],
                                start=(fc == 0), stop=(fc == 3)),
                                reads=[wkey] + src_keys_fn(f, i), writes=[("ps", b)])
                        P.dve(lambda e, b=b, i=i, n=n: e.tensor_tensor(out=h[:, i, n * 512:(n + 1) * 512],
                                                                         in0=h[:, i, n * 512:(n + 1) * 512], in1=ps[b][:, :],
                                                                         op=ALU.add),
                              reads=[("ps", b), ("h", i)], writes=[("h", i)])

        def mlp_stage(li):
            norm_stage(DEPTH + li)
            t0, nt = gw_tab[("mlp", li)]
            hidT = work[:, 0:8192].rearrange("p (c t) -> p c t", c=8)
            for hb in range(8):
                for a in range(2):
                    wt, wkey = load_gtile(t0 + hb * 4 + a)
                    for fc in range(4):
                        hc = a * 4 + fc
                        for th in range(2):
                            b = bank()
                            for kc in range(16):
                                P.pe(lambda e, b=b, kc=kc, fc=fc, th=th, wt=wt: e.matmul(
                                    ps[b][:, :], lhsT=wt[:, kc * 512 + fc * 128: kc * 512 + (fc + 1) * 128],
                                    rhs=hnT[:, kc, th * 512:(th + 1) * 512], start=(kc == 0), stop=(kc == 15)),
                                    reads=[wkey] + hnT_keys(kc, range(th * 4, th * 4 + 4)), writes=[("ps", b)])
                            rt = relu_t[(hc * 2 + th) % 2]
                            rk = ("relu", (hc * 2 + th) % 2)
                            P.act(lambda e, b=b, rt=rt: e.activation(out=rt[:], in_=ps[b][:, :], func=AF.Relu),
                                  reads=[("ps", b)], writes=[rk])
                            P.dve(lambda e, rt=rt, hc=hc, th=th: e.tensor_tensor(out=hidT[:, hc, th * 512:(th + 1) * 512],
                                                                                   in0=rt[:], in1=rt[:], op=ALU.mult),
                                  reads=[rk], writes=wk(hc * 1024 + th * 512, 512))
                proj_out(lambda f, i: hidT[:, f, i * 128:(i + 1) * 128],
                         lambda f, i: wk(f * 1024 + (i // 4) * 512, 512),
                         [t0 + hb * 4 + 2, t0 + hb * 4 + 3])

        def exchange_hn():
            P.dma("sp", "hxw", lambda e: e.dma_start(out=hx_b.rearrange("(c p) t -> p c t", p=128), in_=hnT[:, :, :]),
                  reads=hnT_keys(), writes=["hxb"])
            for a in range(8):
                P.dma("pool", "cc", lambda e, a=a: e.collective_compute(
                    "AllGather", ALU.bypass, replica_groups=[[0, 1, 2, 3], [4, 5, 6, 7]],
                    ins=[hx_b32[a * 256:(a + 1) * 256, :]], outs=[hx_g32[a * 1024:(a + 1) * 1024, :]]),
                    reads=["hxb"], writes=[("hxg", a)], inc=1)

        hx_view = hx_g.rearrange("(a r k p) t -> r p a k t", a=8, r=4, k=2, p=128)

        def load_hn_block(tb, slot):
            r, half = tb // 2, tb % 2
            blk = hnT[:, :, :].rearrange("p c (s t) -> p s c t", s=2)[:, slot]
            for kl in range(2):
                dst = blk.rearrange("p (a k) t -> p k a t", k=2)[:, kl]
                src = hx_view[r][:, :, kl, half * 512:(half + 1) * 512]
                P.dma("sp", ("hnb", slot), lambda e, dst=dst, src=src: e.dma_start(out=dst, in_=src),
                      reads=[("hxg", a) for a in range(8)],
                      writes=[("hnb", slot)] + hnT_keys(None, range(4 * slot, 4 * slot + 4)))
            return blk

        def return_exchange(oT_all):
            for a in range(8):
                P.dma("pool", "cc", lambda e, a=a: e.collective_compute(
                    "AllGather", ALU.bypass, replica_groups=[[0, 1, 2, 3], [4, 5, 6, 7]],
                    ins=[ob_b32[a * 64:(a + 1) * 64, :]], outs=[ob_g32[a * 256:(a + 1) * 256, :]]),
                    reads=[("obb", a // 2)], writes=[("obg", a)], inc=1)
            og_rows = ob_g.rearrange("r (j t) -> (r j) t", j=4)
            for f in range(16):
                P.dma("pool", ("ogath", f), lambda e, f=f: e.indirect_dma_start(
                    out=oT_all[:, f, :], out_offset=None, in_=og_rows,
                    in_offset=bass.IndirectOffsetOnAxis(ap=idxt[:, f:f + 1], axis=0)),
                    reads=["idxt"] + [("obg", a) for a in range(8)], writes=[("oT", f)] + hnT_keys(f))

        class Carver:
            def __init__(self):
                self.off = 0

            def alloc(self, n, dt=BF16):
                units = n * (2 if dt == F32 else 1)
                units_al = (units + 511) // 512 * 512
                off = self.off
                self.off += units_al
                assert self.off <= 32768, self.off
                ap = work[:, off:off + units]
                if dt == F32:
                    ap = ap.bitcast(F32)
                mul = 2 if dt == F32 else 1

                def keys(lo=0, hi=n):
                    return wk(off + lo * mul, (hi - lo) * mul)
                return ap, keys

        def attention_stage(li, ai, hook):
            scale = 128.0 ** -0.5
            norm_stage(li)
            ckpt(1)
            exchange_hn()
            ckpt(2)
            hook()
            ckpt(3)
            gq = cols[:, 2 * (li // 3):2 * (li // 3) + 1]
            gk = cols[:, 2 * (li // 3) + 1:2 * (li // 3) + 2]
            cv = Carver()
            qT, k_qT = cv.alloc(4096)
            kT, k_kT = cv.alloc(4096)
            vRf, k_vR = cv.alloc(4096)
            vR = vRf.rearrange("p (t d) -> p t d", t=32)
            oTh, k_oTh = cv.alloc(4096)
            sqb = [cv.alloc(512) for _ in range(2)]
            vtmp, k_vtmp = cv.alloc(512)
            a_bf = [cv.alloc(512) for _ in range(2)]
            aT = [cv.alloc(512) for _ in range(2)]
            rstd = [cv.alloc(512, F32) for _ in range(2)]
            ombs = [cv.alloc(768, F32) for _ in range(2)]
            beta = [cv.alloc(512, F32) for _ in range(2)]
            pex = [cv.alloc(512, F32) for _ in range(2)]
            for hd in range(4):
                wrow = (ai * 4 + hd) * 128
                s = wslot_ctr[0] % NWS
                wslot_ctr[0] += 1
                wq = wsl[s]
                wkey = ("w", s)
                P.dma("sp", wkey, lambda e, wrow=wrow, wq=wq: e.dma_start(out=wq[:, 0:6144], in_=wqkv_bf[wrow:wrow + 128, :]),
                      reads=[("wqkvb", ai * 4 + hd, q) for q in range(3)], writes=[wkey])
                for tb in range(8):
                    hb = load_hn_block(tb, tb % 2)
                    hkey = ("hnb", tb % 2)
                    for which in range(2):
                        b = bank((0, 1, 2, 3))
                        for kc in range(16):
                            P.pe(lambda e, b=b, kc=kc, which=which, hb=hb, wq=wq: e.matmul(
                                ps[b][:, :], lhsT=wq[:, kc * 384 + which * 128: kc * 384 + (which + 1) * 128],
                                rhs=hb[:, kc, :], start=(kc == 0), stop=(kc == 15)),
                                reads=[wkey, hkey], writes=[("ps", b)])
                        sq, k_sq = sqb[which]
                        P.act(lambda e, b=b, sq=sq: e.activation(out=sq, in_=ps[b][:, :], func=AF.Square),
                              reads=[("ps", b)], writes=k_sq())
                        b2 = bank((0, 1, 2, 3))
                        P.pe(lambda e, b2=b2, sq=sq: e.matmul(ps[b2][:, :], lhsT=ones_bf[:], rhs=sq, start=True, stop=True),
                             reads=k_sq() + ["onesbf"], writes=[("ps", b2)])
                        rt, k_rt = rstd[which]
                        P.act(lambda e, b2=b2, rt=rt: e.activation(out=rt, in_=ps[b2][:, :], func=AF.Sqrt, scale=1.0 / 128,
                                                                   bias=epsc[:, 0:1]),
                              reads=[("ps", b2), "epsc"], writes=k_rt())
                        P.dve(lambda e, rt=rt: e.reciprocal(out=rt, in_=rt), reads=k_rt(), writes=k_rt())
                        if which == 0:
                            dst = qT[:, tb * 512:(tb + 1) * 512]
                            gcol = gq
                            wr = k_qT(tb * 512, (tb + 1) * 512)
                        else:
                            hi = SEQ - 1 - tb * 512
                            lo = hi - 512
                            dst = kT[:, hi:lo:-1] if lo >= 0 else kT[:, hi::-1]
                            gcol = gk
                            wr = k_kT(lo + 1, hi + 1)
                        P.dve(lambda e, b=b, rt=rt, dst=dst, gcol=gcol: e.scalar_tensor_tensor(
                            out=dst, in0=ps[b][:, :], scalar=gcol, in1=rt, op0=ALU.mult, op1=ALU.mult),
                            reads=[("ps", b), "cols"] + k_rt(), writes=wr)
                    b = bank((0, 1, 2, 3))
                    for tt in range(4):
                        for kc in range(16):
                            P.pe(lambda e, b=b, kc=kc, tt=tt, hb=hb, wq=wq: e.matmul(
                                ps[b][:, tt * 128:(tt + 1) * 128], lhsT=hb[:, kc, tt * 128:(tt + 1) * 128],
                                rhs=wq[:, kc * 384 + 256: kc * 384 + 384], start=(kc == 0), stop=(kc == 15)),
                                reads=[wkey, hkey], writes=[("ps", b)])
                    P.act(lambda e, b=b: e.activation(out=vtmp, in_=ps[b][:, :], func=AF.Copy),
                          reads=[("ps", b)], writes=k_vtmp())
                    b2 = bank((0, 1, 2, 3))
                    P.pe(lambda e, b2=b2: e.matmul(ps[b2][:, :], lhsT=jrev[:], rhs=vtmp, start=True, stop=True),
                         reads=k_vtmp() + ["jrev"], writes=[("ps", b2)])
                    for tt in range(4):
                        gi = tb * 4 + tt
                        tr = 31 - gi
                        evac_copy(vR[:, tr, :], ps[b2][:, tt * 128:(tt + 1) * 128], [("ps", b2)],
                                  k_vR(tr * 128, (tr + 1) * 128))
                ckpt(4)
                chunk_ctr = 0
                for i in range(32):
                    if i == 2:
                        ckpt(5)
                    c_start = (31 - i) * 128
                    nblk = i + 1
                    bo = 4 + (i % 2)
                    okey = ("ps", bo)
                    done = 0
                    first = True
                    prev = None
                    while done < nblk:
                        nb = min(4, nblk - done)
                        n = nb * 128
                        c0 = c_start + done * 128
                        u = chunk_ctr % 2
                        chunk_ctr += 1
                        bs = 6 + u
                        (ob, k_ob), (be, k_be), (px, k_px), (ab, k_ab), (at, k_at) = ombs[u], beta[u], pex[u], a_bf[u], aT[u]
                        P.pe(lambda e, bs=bs, i=i, c0=c0, n=n: e.matmul(ps[bs][:, 0:n], lhsT=qT[:, i * 128:(i + 1) * 128],
                                                                          rhs=kT[:, c0:c0 + n], start=True, stop=True),
                             reads=k_qT(i * 128, (i + 1) * 128) + k_kT(c0, c0 + n), writes=[("ps", bs)])
                        P.act(lambda e, bs=bs, be=be, n=n: e.activation(out=be[:, 0:n], in_=ps[bs][:, 0:n], func=AF.Sigmoid,
                                                                         scale=scale),
                              reads=[("ps", bs)], writes=k_be())
                        if first:
                            P.dve(lambda e, be=be: e.tensor_tensor(out=be[:, 0:128], in0=be[:, 0:128], in1=mdiag[:], op=ALU.mult),
                                  reads=k_be() + ["mdiag"], writes=k_be())
                            P.dve(lambda e, be=be, ob=ob: e.tensor_scalar(out=ob[:, 1:129], in0=be[:, 0:128], scalar1=-1.0,
                                                                          scalar2=1.0, op0=ALU.mult, op1=ALU.add),
                                  reads=k_be(), writes=k_ob())
                            P.dve(lambda e, ob=ob: e.memset(ob[:, 0:1], 1.0), writes=k_ob())
                            if n > 128:
                                P.act(lambda e, bs=bs, ob=ob, n=n: e.activation(out=ob[:, 129:n + 1], in_=ps[bs][:, 128:n],
                                                                                 func=AF.Sigmoid, scale=-scale),
                                      reads=[("ps", bs)], writes=k_ob())
                            init = 1.0
                            rd_init = []
                        else:
                            P.act(lambda e, bs=bs, ob=ob, n=n: e.activation(out=ob[:, 1:n + 1], in_=ps[bs][:, 0:n],
                                                                             func=AF.Sigmoid, scale=-scale),
                                  reads=[("ps", bs)], writes=k_ob())
                            pu, pn = prev
                            P.dve(lambda e, ob=ob, pu=pu, pn=pn: e.tensor_copy(out=ob[:, 0:1], in_=ombs[pu][0][:, pn:pn + 1]),
                                  reads=ombs[pu][1](), writes=k_ob())
                            init = pex[pu][0][:, pn - 1:pn]
                            rd_init = pex[pu][1]()
                        P.dve(lambda e, px=px, ob=ob, n=n, init=init: e.tensor_tensor_scan(
                            out=px[:, 0:n], data0=ob[:, 0:n], data1=ones_f[:, 0:n], initial=init, op0=ALU.mult, op1=ALU.mult),
                            reads=k_ob() + ["onesf"] + rd_init, writes=k_px())
                        P.dve(lambda e, ab=ab, px=px, be=be, n=n: e.tensor_tensor(out=ab[:, 0:n], in0=px[:, 0:n], in1=be[:, 0:n],
                                                                                  op=ALU.mult),
                              reads=k_px() + k_be(), writes=k_ab())
                        bt = bank((0, 1, 2, 3))
                        for jj in range(nb):
                            P.pe(lambda e, bt=bt, jj=jj, ab=ab: e.matmul(ps[bt][:, jj * 128:(jj + 1) * 128],
                                                                          lhsT=ab[:, jj * 128:(jj + 1) * 128], rhs=ident[:],
                                                                          start=True, stop=True),
                                 reads=k_ab() + ["ident"], writes=[("ps", bt)])
                        P.act(lambda e, bt=bt, at=at, n=n: e.activation(out=at[:, 0:n], in_=ps[bt][:, 0:n], func=AF.Copy),
                              reads=[("ps", bt)], writes=k_at())
                        for jj in range(nb):
                            blk = done + jj
                            tl = c0 // 128 + jj
                            P.pe(lambda e, bo=bo, jj=jj, at=at, tl=tl, blk=blk, nblk=nblk: e.matmul(
                                ps[bo][:, 0:128], lhsT=vR[:, tl, :], rhs=at[:, jj * 128:(jj + 1) * 128],
                                start=(blk == 0), stop=(blk == nblk - 1)),
                                reads=k_at() + k_vR(tl * 128, (tl + 1) * 128), writes=[okey])
                        prev = (u, n)
                        first = False
                        done += nb
                    evac_copy(oTh[:, i * 128:(i + 1) * 128], ps[bo][:, 0:128], [okey], k_oTh(i * 128, (i + 1) * 128))
                P.dma("sp", "obw", lambda e, hd=hd: e.dma_start(out=ob_b[hd * 128:(hd + 1) * 128, :], in_=oTh),
                      reads=k_oTh(), writes=[("obb", hd)])
            ckpt(6)
            return_exchange(hnT)
            ckpt(7)
            t0, nt = gw_tab[("wo", li)]
            proj_out(lambda f, i: hnT[:, f, i * 128:(i + 1) * 128],
                     lambda f, i: [("oT", f), ("hnT", f, i)],
                     list(range(t0, t0 + nt)))

        def gmlp_stage(li, hook):
            norm_stage(li)
            hook()
            cv = Carver()
            u, k_u = cv.alloc(8192)
            v, k_v = cv.alloc(8192)
            wsf, k_wsf = cv.alloc(2048, F32)
            wsT, k_wsT = cv.alloc(2048)
            u3 = u.rearrange("p (i d) -> p i d", i=4)
            v3 = v.rearrange("p (i d) -> p i d", i=4)
            wsf3 = wsf.rearrange("p (g t) -> p g t", g=16)
            P.dma("sp", "gws", lambda e: e.dma_start(out=wsf, in_=gws_in[:, :]), writes=k_wsf())
            P.op("pool", lambda e: e.affine_select(out=wsf3, in_=wsf3, pattern=[[0, 16], [1, 128]], compare_op=ALU.is_ge,
                                                   fill=0.0, base=0, channel_multiplier=-1), reads=k_wsf(), writes=k_wsf())
            P.dve(lambda e: e.tensor_copy(out=wsT, in_=wsf), reads=k_wsf(), writes=k_wsT())
            P.dma("sp", "gbc", lambda e: e.dma_start(out=gbc, in_=gvec[8:9, :].partition_broadcast(128)), writes=k_gbc)
            t0g, _ = gw_tab[("gin", li)]
            for half in range(2):
                for cg in range(8):
                    wt, wkey = load_gtile(t0g + cg)
                    for il in range(4):
                        i = half * 4 + il
                        b = bank()
                        for kc in range(16):
                            P.pe(lambda e, b=b, kc=kc, i=i, wt=wt: e.matmul(ps[b][:, :], lhsT=hnT[:, kc, i * 128:(i + 1) * 128],
                                                                             rhs=wt[:, kc * 512:(kc + 1) * 512],
                                                                             start=(kc == 0), stop=(kc == 15)),
                                 reads=[wkey, ("hnT", kc, i)], writes=[("ps", b)])
                        tgt, ktg = (u3, k_u) if cg < 4 else (v3, k_v)
                        c4 = cg % 4
                        P.act(lambda e, b=b, tgt=tgt, il=il, c4=c4: e.activation(out=tgt[:, il, c4 * 512:(c4 + 1) * 512],
                                                                                  in_=ps[b][:, :], func=AF.Gelu),
                              reads=[("ps", b)], writes=ktg(il * 2048 + c4 * 512, il * 2048 + (c4 + 1) * 512))
                for il in range(4):
                    i = half * 4 + il
                    kv = k_v(il * 2048, (il + 1) * 2048)
                    ku = k_u(il * 2048, (il + 1) * 2048)
                    sc = 8 + il
                    P.act(lambda e, il=il, sc=sc: e.activation(out=junk, in_=v3[:, il, :], func=AF.Square,
                                                               accum_out=ss[:, sc:sc + 1]),
                          reads=kv, writes=[("ss", sc)] + k_junk)
                    P.act(lambda e, sc=sc: e.activation(out=rs[:, sc:sc + 1], in_=ss[:, sc:sc + 1], func=AF.Sqrt, scale=1.0 / D,
                                                        bias=epsc[:, 0:1]), reads=[("ss", sc), "epsc"], writes=[("rs", sc)])
                    P.dve(lambda e, sc=sc: e.reciprocal(out=rs[:, sc:sc + 1], in_=rs[:, sc:sc + 1]),
                          reads=[("rs", sc)], writes=[("rs", sc)])
                    P.dve(lambda e, il=il, sc=sc: e.scalar_tensor_tensor(out=v3[:, il, :], in0=v3[:, il, :], scalar=rs[:, sc:sc + 1],
                                                                         in1=gbc, op0=ALU.mult, op1=ALU.mult),
                          reads=kv + [("rs", sc)] + k_gbc, writes=kv)
                    for gq in range(4):
                        b = bank()
                        for gg in range(4):
                            g = gq * 4 + gg
                            P.pe(lambda e, b=b, gg=gg, g=g, il=il: e.matmul(ps[b][:, gg * 128:(gg + 1) * 128],
                                                                             lhsT=wsT[:, g * 128:(g + 1) * 128],
                                                                             rhs=v3[:, il, g * 128:(g + 1) * 128], start=True, stop=True),
                                 reads=k_wsT() + kv, writes=[("ps", b)])
                        for gg in range(4):
                            g = gq * 4 + gg
                            P.dve(lambda e, b=b, gg=gg, g=g, il=il: e.scalar_tensor_tensor(
                                out=u3[:, il, g * 128:(g + 1) * 128], in0=ps[b][:, gg * 128:(gg + 1) * 128],
                                scalar=cols[:, 4 + g:5 + g], in1=u3[:, il, g * 128:(g + 1) * 128], op0=ALU.add, op1=ALU.mult),
                                reads=[("ps", b), "cols"] + ku, writes=ku)
                    for cq in range(4):
                        b = bank()
                        for c in range(4):
                            cc = cq * 4 + c
                            P.pe(lambda e, b=b, c=c, cc=cc, il=il: e.matmul(ps[b][:, c * 128:(c + 1) * 128],
                                                                             lhsT=u3[:, il, cc * 128:(cc + 1) * 128], rhs=ident[:],
                                                                             start=True, stop=True),
                                 reads=ku + ["ident"], writes=[("ps", b)])
                        evac_copy(hnT[:, cq * 4:(cq + 1) * 4, i * 128:(i + 1) * 128],
                                  ps[b][:, :].rearrange("p (c t) -> p c t", c=4),
                                  [("ps", b)], [("hnT", cq * 4 + c2, i) for c2 in range(4)])
            t0, nt = gw_tab[("wo", li)]
            proj_out(lambda f, i: hnT[:, f, i * 128:(i + 1) * 128],
                     lambda f, i: [("hnT", f, i)], list(range(t0, t0 + nt)))

        class Carver2:
            def __init__(self):
                self.off = 0

            def alloc(self, n, dt=BF16):
                units = n * (2 if dt == F32 else 1)
                mul = 2 if dt == F32 else 1
                al = (units + 511) // 512 * 512
                if self.off < 32768 and self.off + al > 32768:
                    self.off = 32768
                off = self.off
                self.off += al
                assert self.off <= 32768 + 16384, self.off
                if off < 32768:
                    ap = work[:, off:off + units]

                    def keys(lo=0, hi=n, off=off):
                        return wk(off + lo * mul, (hi - lo) * mul)
                else:
                    o2 = off - 32768
                    flat = hnT[:, :, :].rearrange("p c t -> p (c t)")
                    ap = flat[:, o2:o2 + units]

                    def keys(lo=0, hi=n, o2=o2):
                        a = o2 + lo * mul
                        bnd = o2 + hi * mul
                        return [("hnT", g // 8, g % 8) for g in range(a // 128, (bnd - 1) // 128 + 1)]
                if dt == F32:
                    ap = ap.bitcast(F32)
                return ap, keys

        def ssd_stage(li, hook):
            CB = 20
            norm_stage(li)
            exchange_hn()
            hook()
            stg = [work[:, i * 1024:(i + 1) * 1024].bitcast(F32) for i in range(2)]
            k_stg = [wk(i * 1024, 1024) for i in range(2)]
            sctr = 0
            for wi in range(6):
                s_ = wslot_ctr[0] % NWS
                wslot_ctr[0] += 1
                wt = wsl[s_]
                wkey = ("w", s_)
                P.dma("sp", wkey, lambda e, wi=wi, wt=wt: e.dma_start(out=wt[:, :], in_=wssd_bf[wi * 128:(wi + 1) * 128, :]),
                      reads=[("wssdb", wi, q) for q in range(4)], writes=[wkey])
                for tb in range(8):
                    hb = load_hn_block(tb, tb % 2)
                    hkey = ("hnb", tb % 2)
                    for fc in range(4 if wi < 5 else 1):
                        M = 128 if wi < 5 else 16
                        fcid = wi * 4 + fc
                        b = bank()
                        for kc in range(16):
                            P.pe(lambda e, b=b, kc=kc, fc=fc, M=M, hb=hb, wt=wt: e.matmul(
                                ps[b][0:M, :], lhsT=wt[:, kc * 512 + fc * 128: kc * 512 + fc * 128 + M], rhs=hb[:, kc, :],
                                start=(kc == 0), stop=(kc == 15)), reads=[wkey, hkey], writes=[("ps", b)])
                        u_ = sctr % 2
                        sctr += 1
                        evac_copy(stg[u_][0:M, :], ps[b][0:M, :], [("ps", b)], k_stg[u_])
                        P.dma("sp", ("zxw", u_), lambda e, u_=u_, M=M, fcid=fcid, tb=tb: e.dma_start(
                            out=zx[fcid * 128: fcid * 128 + M, tb * 512:(tb + 1) * 512], in_=stg[u_][0:M, :]),
                            reads=k_stg[u_], writes=[("zx", fcid)])
            cv = Carver2()
            BT, k_BT = cv.alloc(4096)
            CT, k_CT = cv.alloc(4096)
            pre, k_pre = cv.alloc(4100, F32)
            acc, k_acc = cv.alloc(4096, F32)
            xTj, k_xT = cv.alloc(4096)
            xsj, k_xs = cv.alloc(4096)
            dtok, k_dtok = cv.alloc(2048, F32)
            Acol, k_Acol = cv.alloc(512, F32)
            dg = [cv.alloc(128, F32) for _ in range(2)]
            Eb = [cv.alloc(512, F32) for _ in range(2)]
            Wb = [cv.alloc(512) for _ in range(2)]
            WTb = [cv.alloc(512) for _ in range(2)]
            smalls, k_sm = cv.alloc(256, F32)
            consts, k_consts = cv.alloc(512, F32)
            identf, onesf2, l01, mtri = (consts[:, q * 128:(q + 1) * 128] for q in range(4))
            k_idf = k_of2 = k_l01 = k_mtri = k_consts
            xs3 = xsj.rearrange("p (t c) -> p t c", t=32)
            dtok3 = dtok.rearrange("p (t c) -> p t c", t=32)
            Acol3 = Acol.rearrange("p (t c) -> p t c", t=32)
            negA = pre[:, 0:4096]
            P.dve(lambda e: e.memset(onesf2, 1.0), writes=k_of2())
            P.op("pool", lambda e: e.affine_select(out=identf, in_=onesf2, pattern=[[-1, 128]], compare_op=ALU.is_equal,
                                                   fill=0.0, base=0, channel_multiplier=1), reads=k_of2(), writes=k_idf())
            P.op("pool", lambda e: e.affine_select(out=l01, in_=onesf2, pattern=[[1, 128]], compare_op=ALU.is_ge,
                                                   fill=0.0, base=0, channel_multiplier=-1), reads=k_of2(), writes=k_l01())
            P.op("pool", lambda e: e.affine_select(out=mtri, in_=onesf2, pattern=[[-1, 128]], compare_op=ALU.is_ge,
                                                   fill=0.0, base=0, channel_multiplier=1), reads=k_of2(), writes=k_mtri())
            P.dma("sp", "zxr", lambda e: e.dma_start(out=pre[0:16, 0:4096], in_=zx[20 * 128:20 * 128 + 16, :]),
                  reads=[("zx", 20)], writes=k_pre())
            P.act(lambda e: e.activation(out=pre[0:16, 0:4096], in_=pre[0:16, 0:4096], func=AF.Exp, bias=cols[0:16, CB + 60:CB + 61]),
                  reads=k_pre() + ["cols"], writes=k_pre())
            P.act(lambda e: e.activation(out=acc[0:16, :], in_=pre[0:16, 0:4096], func=AF.Ln, bias=onesf2[0:16, 0:1]),
                  reads=k_pre() + k_of2(), writes=k_acc())
            P.act(lambda e: e.activation(out=smalls[0:16, 0:1], in_=cols[0:16, CB + 61:CB + 62], func=AF.Exp),
                  reads=["cols"], writes=k_sm())
            P.dve(lambda e: e.tensor_scalar(out=smalls[0:16, 0:1], in0=smalls[0:16, 0:1], scalar1=-1.0, scalar2=None, op0=ALU.mult),
                  reads=k_sm(), writes=k_sm())
            P.dve(lambda e: e.tensor_scalar(out=pre[0:16, 0:4096], in0=acc[0:16, :], scalar1=smalls[0:16, 0:1], scalar2=None,
                                            op0=ALU.mult), reads=k_acc() + k_sm(), writes=k_pre())
            for i in range(32):
                b = bank((0, 1, 2, 3))
                P.pe(lambda e, b=b, i=i: e.matmul(ps[b][:, 0:16], lhsT=acc[0:16, i * 128:(i + 1) * 128], rhs=identf[0:16, 0:16],
                                                   start=True, stop=True), reads=k_acc() + k_idf(), writes=[("ps", b)])
                P.pe(lambda e, b=b, i=i: e.matmul(ps[b][:, 32:48], lhsT=pre[0:16, i * 128:(i + 1) * 128], rhs=identf[0:16, 0:16],
                                                   start=True, stop=True), reads=k_pre() + k_idf(), writes=[("ps", b)])
                evac_copy(dtok3[:, i, 0:48], ps[b][:, 0:48], [("ps", b)], k_dtok(i * 64, (i + 1) * 64))
            for i in range(32):
                b = bank((0, 1, 2, 3))
                for j in range(i):
                    P.pe(lambda e, b=b, j=j: e.matmul(ps[b][:, 0:16], lhsT=onesf2, rhs=dtok3[:, j, 32:48], start=(j == 0), stop=False),
                         reads=k_of2() + k_dtok(), writes=[("ps", b)])
                P.pe(lambda e, b=b, i=i: e.matmul(ps[b][:, 0:16], lhsT=l01, rhs=dtok3[:, i, 32:48], start=(i == 0), stop=True),
                     reads=k_l01() + k_dtok(), writes=[("ps", b)])
                evac_copy(Acol3[:, i, :], ps[b][:, 0:16], [("ps", b)], k_Acol(i * 16, (i + 1) * 16))

            def conv_chunk(fcid, ccol, dst, k_dst):
                P.dve(lambda e: e.memset(pre[:, 0:4], 0.0), writes=k_pre(0, 4))
                P.dma("sp", "zxr", lambda e: e.dma_start(out=pre[:, 3:4099], in_=zx[fcid * 128:(fcid + 1) * 128, :]),
                      reads=[("zx", fcid)], writes=k_pre())
                P.dve(lambda e: e.tensor_scalar(out=acc, in0=pre[:, 0:4096], scalar1=cols[:, CB + ccol * 4:CB + ccol * 4 + 1],
                                                scalar2=None, op0=ALU.mult), reads=k_pre() + ["cols"], writes=k_acc())
                for tp in range(1, 4):
                    P.dve(lambda e, tp=tp: e.scalar_tensor_tensor(out=acc, in0=pre[:, tp:tp + 4096],
                                                                  scalar=cols[:, CB + ccol * 4 + tp:CB + ccol * 4 + tp + 1],
                                                                  in1=acc, op0=ALU.mult, op1=ALU.add),
                          reads=k_pre() + k_acc() + ["cols"], writes=k_acc())
                P.act(lambda e: e.activation(out=dst, in_=acc, func=AF.Silu, bias=cols[:, CB + 48 + ccol:CB + 49 + ccol]),
                      reads=k_acc() + ["cols"], writes=k_dst())

            yT = acc
            for gl in range(2):
                conv_chunk(16 + gl, 8 + gl, BT, k_BT)
                conv_chunk(18 + gl, 10 + gl, CT, k_CT)
                for jj in range(4):
                    j = gl * 4 + jj
                    conv_chunk(8 + j, j, xTj, k_xT)
                    for i in range(32):
                        b = bank((0, 1, 2, 3))
                        P.pe(lambda e, b=b, i=i: e.matmul(ps[b][:, 0:128], lhsT=xTj[:, i * 128:(i + 1) * 128], rhs=ident[:],
                                                           start=True, stop=True), reads=k_xT() + ["ident"], writes=[("ps", b)])
                        for hh in range(2):
                            hl = 2 * j + hh
                            P.dve(lambda e, b=b, i=i, hh=hh, hl=hl: e.tensor_scalar(
                                out=xs3[:, i, hh * 64:(hh + 1) * 64], in0=ps[b][:, hh * 64:(hh + 1) * 64],
                                scalar1=dtok3[:, i, hl:hl + 1], scalar2=None, op0=ALU.mult),
                                reads=[("ps", b)] + k_dtok(), writes=k_xs(i * 128, (i + 1) * 128))
                    for hh in range(2):
                        hl = 2 * j + hh
                        for i4 in range(8):
                            b = bank((0, 1, 2, 3))
                            for q in range(4):
                                i = i4 * 4 + q
                                dgt, k_dg = dg[i % 2]
                                P.dve(lambda e, dgt=dgt, i=i, hl=hl: e.tensor_scalar(out=dgt, in0=identf, scalar1=Acol3[:, i, hl:hl + 1],
                                                                                     scalar2=None, op0=ALU.mult),
                                      reads=k_idf() + k_Acol(), writes=k_dg())
                                P.pe(lambda e, b=b, q=q, dgt=dgt: e.matmul(ps[b][:, q * 128:(q + 1) * 128], lhsT=onesf2, rhs=dgt,
                                                                           start=True, stop=True),
                                     reads=k_dg() + k_of2(), writes=[("ps", b)])
                            P.act(lambda e, b=b, i4=i4: e.activation(out=negA[:, i4 * 512:(i4 + 1) * 512], in_=ps[b][:, :],
                                                                      func=AF.Copy, scale=-1.0),
                                  reads=[("ps", b)], writes=k_pre(i4 * 512, (i4 + 1) * 512))
                        cctr = 0
                        for i in range(32):
                            nblk = i + 1
                            bo = 4 + (i % 2)
                            okey = ("ps", bo)
                            acol = Acol3[:, i, hl:hl + 1]
                            for c in range(0, nblk, 4):
                                nb = min(4, nblk - c)
                                n = nb * 128
                                has_diag = (c + nb == nblk)
                                u_ = cctr % 2
                                cctr += 1
                                bs = 6 + u_
                                (E, k_E), (W_, k_W), (WT, k_WT) = Eb[u_], Wb[u_], WTb[u_]
                                P.pe(lambda e, bs=bs, i=i, c=c, n=n: e.matmul(ps[bs][:, 0:n], lhsT=CT[:, i * 128:(i + 1) * 128],
                                                                               rhs=BT[:, c * 128:c * 128 + n], start=True, stop=True),
                                     reads=k_CT(i * 128, (i + 1) * 128) + k_BT(c * 128, c * 128 + n), writes=[("ps", bs)])
                                n1 = n - 128 if has_diag else n
                                if n1 > 0:
                                    P.act(lambda e, E=E, c=c, n1=n1, acol=acol: e.activation(
                                        out=E[:, 0:n1], in_=negA[:, c * 128:c * 128 + n1], func=AF.Exp, bias=acol),
                                        reads=k_pre(c * 128, c * 128 + n1) + k_Acol(), writes=k_E())
                                if has_diag:
                                    P.dve(lambda e, E=E, i=i, n=n, acol=acol: e.tensor_scalar(
                                        out=E[:, n - 128:n], in0=negA[:, i * 128:(i + 1) * 128], scalar1=acol, scalar2=0.0,
                                        op0=ALU.add, op1=ALU.min), reads=k_pre(i * 128, (i + 1) * 128) + k_Acol(), writes=k_E())
                                    P.act(lambda e, E=E, n=n: e.activation(out=E[:, n - 128:n], in_=E[:, n - 128:n], func=AF.Exp),
                                          reads=k_E(), writes=k_E())
                                    P.dve(lambda e, E=E, n=n: e.tensor_tensor(out=E[:, n - 128:n], in0=E[:, n - 128:n], in1=mtri,
                                                                              op=ALU.mult), reads=k_E() + k_mtri(), writes=k_E())
                                P.dve(lambda e, W_=W_, E=E, bs=bs, n=n: e.tensor_tensor(out=W_[:, 0:n], in0=E[:, 0:n], in1=ps[bs][:, 0:n],
                                                                                         op=ALU.mult),
                                      reads=k_E() + [("ps", bs)], writes=k_W())
                                bt = bank((0, 1, 2, 3))
                                for q in range(nb):
                                    P.pe(lambda e, bt=bt, q=q, W_=W_: e.matmul(ps[bt][:, q * 128:(q + 1) * 128],
                                                                               lhsT=W_[:, q * 128:(q + 1) * 128], rhs=ident[:],
                                                                               start=True, stop=True),
                                         reads=k_W() + ["ident"], writes=[("ps", bt)])
                                evac_copy(WT[:, 0:n], ps[bt][:, 0:n], [("ps", bt)], k_WT())
                                for q in range(nb):
                                    blk = c + q
                                    P.pe(lambda e, bo=bo, q=q, WT=WT, blk=blk, nblk=nblk, hh=hh: e.matmul(
                                        ps[bo][hh * 64:(hh + 1) * 64, 0:128], lhsT=xs3[:, blk, hh * 64:(hh + 1) * 64],
                                        rhs=WT[:, q * 128:(q + 1) * 128], start=(blk == 0), stop=(blk == nblk - 1)),
                                        reads=k_WT() + k_xs(blk * 128, (blk + 1) * 128), writes=[okey])
                            evac_copy(yT[hh * 64:(hh + 1) * 64, i * 128:(i + 1) * 128], ps[bo][hh * 64:(hh + 1) * 64, 0:128],
                                      [okey], k_acc(i * 128, (i + 1) * 128))
                    P.dve(lambda e, j=j: e.scalar_tensor_tensor(out=yT, in0=xTj, scalar=cols[:, CB + 62 + j:CB + 63 + j], in1=yT,
                                                                op0=ALU.mult, op1=ALU.add),
                          reads=k_xT() + k_acc() + ["cols"], writes=k_acc())
                    P.dma("sp", "zxr", lambda e, j=j: e.dma_start(out=pre[:, 0:4096], in_=zx[j * 128:(j + 1) * 128, :]),
                          reads=[("zx", j)], writes=k_pre())
                    P.act(lambda e: e.activation(out=pre[:, 0:4096], in_=pre[:, 0:4096], func=AF.Silu), reads=k_pre(), writes=k_pre())
                    P.dve(lambda e: e.tensor_tensor(out=xTj, in0=yT, in1=pre[:, 0:4096], op=ALU.mult),
                          reads=k_acc() + k_pre(), writes=k_xT())
                    P.dma("sp", "ygw", lambda e, j=j: e.dma_start(out=ygd[j * 128:(j + 1) * 128, :], in_=xTj),
                          reads=k_xT(), writes=[("ygd", j)])
                yb, k_yb = Eb[0][0], Eb[0][1]
                for tb in range(8):
                    ybuf = [work[:, 24576 + q * 512:24576 + (q + 1) * 512] for q in range(4)]
                    kyb = wk(24576, 2048)
                    P.dma("sp", "ygr", lambda e, tb=tb, gl=gl: e.dma_start(
                        out=work[:, 24576:24576 + 2048].rearrange("p (q t) -> p q t", q=4),
                        in_=ygd[gl * 512:(gl + 1) * 512, tb * 512:(tb + 1) * 512].rearrange("(q p) t -> p q t", p=128)),
                        reads=[("ygd", gl * 4 + q) for q in range(4)], writes=kyb)
                    b = bank((0, 1, 2, 3))
                    sqt = work[:, 24576 + 2048:24576 + 2560]
                    ksq = wk(24576 + 2048, 512)
                    for q in range(4):
                        P.act(lambda e, q=q: e.activation(out=sqt, in_=ybuf[q], func=AF.Square), reads=kyb, writes=ksq)
                        P.pe(lambda e, b=b, q=q: e.matmul(ps[b][:, :], lhsT=ones_bf[:], rhs=sqt, start=(q == 0), stop=(q == 3)),
                             reads=ksq + ["onesbf"], writes=[("ps", b)])
                    rt = work[:, 24576 + 2560:24576 + 3584].bitcast(F32)
                    krt = wk(24576 + 2560, 1024)
                    P.act(lambda e, b=b: e.activation(out=rt, in_=ps[b][:, :], func=AF.Sqrt, scale=1.0 / 512, bias=epsc[:, 0:1]),
                          reads=[("ps", b), "epsc"], writes=krt)
                    P.dve(lambda e: e.reciprocal(out=rt, in_=rt), reads=krt, writes=krt)
                    for q in range(4):
                        jn = gl * 4 + q
                        P.dve(lambda e, q=q, jn=jn: e.scalar_tensor_tensor(out=ybuf[q], in0=ybuf[q], scalar=cols[:, CB + 70 + jn:CB + 71 + jn],
                                                                           in1=rt, op0=ALU.mult, op1=ALU.mult),
                              reads=kyb + krt + ["cols"], writes=kyb)
                    P.dma("sp", "obw", lambda e, tb=tb, gl=gl: e.dma_start(
                        out=ob_b[gl * 512:(gl + 1) * 512, tb * 512:(tb + 1) * 512].rearrange("(q p) t -> p q t", p=128),
                        in_=work[:, 24576:24576 + 2048].rearrange("p (q t) -> p q t", q=4)),
                        reads=kyb, writes=[("obbs", gl, tb)])
            for a in range(16):
                P.dma("pool", "cc", lambda e, a=a: e.collective_compute(
                    "AllGather", ALU.bypass, replica_groups=[[0, 1, 2, 3], [4, 5, 6, 7]],
                    ins=[ob_b32[a * 64:(a + 1) * 64, :]], outs=[ob_g32[a * 256:(a + 1) * 256, :]]),
                    reads=[("obbs", a // 8, tb) for tb in range(8)], writes=[("obg", a)], inc=1)
            og_rows = ob_g.rearrange("r (j t) -> (r j) t", j=4)
            t0, nt = gw_tab[("wo", li)]
            for hf in range(2):
                for f in range(16):
                    fg = hf * 16 + f
                    P.dma("pool", ("ogath", f), lambda e, f=f, fg=fg: e.indirect_dma_start(
                        out=hnT[:, f, :], out_offset=None, in_=og_rows,
                        in_offset=bass.IndirectOffsetOnAxis(ap=idxt[:, 16 + fg:17 + fg], axis=0)),
                        reads=["idxt"] + [("obg", a) for a in range(16)], writes=[("oT", f)] + hnT_keys(f))
                proj_out(lambda f, i: hnT[:, f, i * 128:(i + 1) * 128],
                         lambda f, i: [("oT", f), ("hnT", f, i)],
                         list(range(t0 + hf * 4, t0 + hf * 4 + 4)))

        mk_const()

        def cast_wqkv(ai):
            for hd in range(4):
                a = ai * 4 + hd
                for q in range(3):
                    P.dma("pool", "cast", lambda e, a=a, q=q: e.dma_start(
                        out=wqkv_bf[a * 128:(a + 1) * 128, q * 2048:(q + 1) * 2048],
                        in_=wqkv_in[a * 128:(a + 1) * 128, q * 2048:(q + 1) * 2048]),
                        writes=[("wqkvb", a, q)])

        def cast_wssd():
            for wi in range(6):
                for q in range(4):
                    P.dma("pool", "cast", lambda e, wi=wi, q=q: e.dma_start(
                        out=wssd_bf[wi * 128:(wi + 1) * 128, q * 2048:(q + 1) * 2048],
                        in_=wssd_in[wi * 128:(wi + 1) * 128, q * 2048:(q + 1) * 2048]),
                        writes=[("wssdb", wi, q)])

        def prep_percore(li):
            if li % 3 == 0:
                cast_wqkv(att_layers.index(li))
            elif li % 3 == 2:
                cast_wssd()

        def prep_layer(li):
            for name in (("gin", li), ("wo", li), ("mlp", li)):
                if name in gw_tab:
                    prep_tiles(*gw_tab[name])

        P.dma("sp", "xin", lambda e: e.dma_start(out=h[:, :, :], in_=x_in.rearrange("(i p) d -> p i d", p=128)),
              writes=[("h", i) for i in range(NT)])
        if layers:
            prep_percore(layers[0])
        for pos, li in enumerate(layers):
            kind = li % 3
            P.epoch = pos + 1
            nxt = layers[pos + 1] if pos + 1 < len(layers) else None

            def hook(li=li):
                prep_layer(li)
            try:
                if kind == 0:
                    attention_stage(li, att_layers.index(li), hook)
                elif kind == 1:
                    gmlp_stage(li, hook)
                else:
                    ssd_stage(li, hook)
                ckpt(8)
                if nxt is not None:
                    prep_percore(nxt)
                    prep_layer(nxt)
                mlp_stage(li)
            except StopBuild:
                break
        od = P.dma("sp", "yout", lambda e: e.dma_start(out=y_out.rearrange("(i p) d -> p i d", p=128), in_=h[:, :, :]),
                   reads=[("h", i) for i in range(NT)], writes=["y"])
        P.emit(final_wait_ops=[od.idx])
    return nc, P


N_COLS = 104


def make_inputs(inp, layers):
    gw = gathered_layout(inp, layers)
    tiles = np.concatenate(gw.chunks, 0)
    if tiles.shape[0] % 2:
        tiles = np.concatenate([tiles, np.zeros_like(tiles[:1])], 0)
    n_gtiles = tiles.shape[0]
    att_layers = [li for li in layers if li % 3 == 0]
    n_att = len(att_layers)
    gvec = np.concatenate([inp["norm_mix_g"], inp["norm_mlp_g"], inp["gm_v_norm_g"]], 0).astype(np.float32)
    gws = np.ascontiguousarray(inp["gm_w_s"][0].transpose(2, 0, 1)).reshape(128, 16 * 128)
    x = inp["x"].reshape(NCORES, TOK, D)
    maps = []
    p = np.arange(128)
    for c in range(NCORES):
        sidx = c % 4
        m = {}
        m["x"] = np.ascontiguousarray(x[c])
        m["wg"] = np.ascontiguousarray(tiles[:, :, c * 1024:(c + 1) * 1024]).reshape(n_gtiles * 128, 1024)
        m["gvec"] = gvec
        cols = np.zeros((128, N_COLS), np.float32)
        for ai in range(2):
            cols[:, 2 * ai] = inp["sb_q_norm_g"][ai]
            cols[:, 2 * ai + 1] = inp["sb_k_norm_g"][ai]
        cols[:, 4:20] = inp["gm_b_s"][0].T
        CB = 20
        cw = inp["ssd_conv_w"][0]
        cb = inp["ssd_conv_b"][0]
        chan = []
        for j in range(8):
            chan.append(sidx * 1024 + j * 128)
        for gl in range(2):
            chan.append(4096 + (2 * sidx + gl) * 128)
        for gl in range(2):
            chan.append(5120 + (2 * sidx + gl) * 128)
        for ccol, ch0 in enumerate(chan):
            for tp in range(4):
                cols[:, CB + ccol * 4 + tp] = cw[tp, ch0:ch0 + 128]
            cols[:, CB + 48 + ccol] = cb[ch0:ch0 + 128]
        cols[0:16, CB + 60] = inp["ssd_dt_bias"][0][sidx * 16:(sidx + 1) * 16]
        cols[0:16, CB + 61] = inp["ssd_a_log"][0][sidx * 16:(sidx + 1) * 16]
        dsk = inp["ssd_d"][0]
        ng = inp["ssd_norm_g"][0]
        for j in range(8):
            cols[0:64, CB + 62 + j] = dsk[sidx * 16 + 2 * j]
            cols[64:128, CB + 62 + j] = dsk[sidx * 16 + 2 * j + 1]
            cols[:, CB + 70 + j] = ng[sidx * 1024 + j * 128: sidx * 1024 + (j + 1) * 128]
        m["cols"] = cols
        idx = np.zeros((128, 48), np.int32)
        for r in range(4):
            for hl in range(4):
                idx[:, r * 4 + hl] = (((hl * 2 + p // 64) * 4 + r) * 64 + p % 64) * 4 + sidx
            for cl in range(8):
                idx[:, 16 + r * 8 + cl] = (((cl * 2 + p // 64) * 4 + r) * 64 + p % 64) * 4 + sidx
        m["idx"] = idx
        wq = np.zeros((max(n_att, 1) * 4 * 128, 6144), np.float32)
        for ai, li in enumerate(att_layers):
            W = inp["sb_w_qkv"][li // 3]
            for hd in range(4):
                hg = sidx * 4 + hd
                blk = np.stack([W[:, hg * 128:(hg + 1) * 128], W[:, 2048 + hg * 128:2048 + (hg + 1) * 128],
                                W[:, 4096 + hg * 128:4096 + (hg + 1) * 128]], 1)
                blk = blk.reshape(16, 128, 384).transpose(1, 0, 2).reshape(128, 6144)
                wq[(ai * 4 + hd) * 128:(ai * 4 + hd + 1) * 128] = blk
        m["wqkv"] = wq
        m["gws"] = gws
        Wi = inp["ssd_w_in"][0]
        sl = np.zeros((2048, 6 * 512), np.float32)
        sl[:, 0:1024] = Wi[:, sidx * 1024:(sidx + 1) * 1024]
        sl[:, 1024:2048] = Wi[:, 4096 + sidx * 1024:4096 + (sidx + 1) * 1024]
        sl[:, 2048:2304] = Wi[:, 8192 + sidx * 256:8192 + (sidx + 1) * 256]
        sl[:, 2304:2560] = Wi[:, 9216 + sidx * 256:9216 + (sidx + 1) * 256]
        sl[:, 2560:2576] = Wi[:, 10240 + sidx * 16:10240 + (sidx + 1) * 16]
        m["wssd"] = tile_weight(sl, 16, 512).reshape(6 * 128, 8192)
        maps.append(m)
    return maps, gw.tab, n_gtiles


_CACHE = {}


def kernel(layers=None, **inp):
    if layers is None:
        layers = list(range(DEPTH))
    inp = {k_: np.asarray(v) for k_, v in inp.items()}
    maps, tab, n_gtiles = make_inputs(inp, layers)
    nc, P = build_program(list(layers), tab, n_gtiles, N_COLS)
    res = run_bass_kernel_spmd(nc, maps, core_ids=list(range(NCORES)))
    out = np.stack([res.results[c]["y"] for c in range(NCORES)], 0)
    return out.reshape(2, SEQ, D).astype(np.float32)
```
